# Optimizing a Trainium2 kernel written in Bass

```python
import math
import jax, jax.numpy as jnp
from jax import lax
import numpy as np

D_MODEL = 1024
BATCH = 16
SEQ = 4096
DEPTH = 1
DEC_BATCH = 1
DEC_SEQ = 16384
PAST_LEN = 128

GLA_HEADS = 4
GLA_DK = 128
GLA_DV = 256
GLA_GATE_RANK = 16
GLA_GATE_NORM = 16.0
GLA_CHUNK = 64
MLA_HEADS = 8
MLA_Q_RANK = 256
MLA_KV_RANK = 128
MLA_NOPE = 128
MLA_ROPE = 64
MLA_V = 128
MLA_QBLOCK = 128
ROPE_THETA = 10000.0
EPS = 1e-6

GLA_QK_W = GLA_HEADS * GLA_DK
GLA_V_W = GLA_HEADS * GLA_DV
MLA_V_W = MLA_HEADS * MLA_V
SPLIT_SIZES = (GLA_QK_W, GLA_QK_W, GLA_V_W, GLA_GATE_RANK, GLA_GATE_RANK, GLA_V_W,
               MLA_Q_RANK, MLA_KV_RANK, MLA_ROPE, MLA_V_W, D_MODEL, D_MODEL)
D_IN = GLA_QK_W * 2 + GLA_V_W * 2 + GLA_GATE_RANK * 2 + MLA_Q_RANK + MLA_KV_RANK + MLA_ROPE + MLA_V_W + 2 * D_MODEL

kernel_name = 'hybrid_gla_mla_gated_encoder'


def rmsnorm(x, g):
    xf = x.astype(jnp.float32)
    y = xf * lax.rsqrt(jnp.mean(xf * xf, axis=-1, keepdims=True) + EPS)
    return (y * g.astype(jnp.float32)).astype(x.dtype)


def split_cols(t, sizes):
    idx = []
    acc = 0
    for s in sizes[:-1]:
        acc += s
        idx.append(acc)
    return jnp.split(t, idx, axis=-1)


def apply_rope(x, pos):
    half = MLA_ROPE // 2
    freqs = ROPE_THETA ** (-jnp.arange(half, dtype=jnp.float32) / half)
    ang = pos.astype(jnp.float32)[:, None] * freqs[None, :]
    cos = jnp.cos(ang)[:, None, :]
    sin = jnp.sin(ang)[:, None, :]
    xf = x.astype(jnp.float32)
    x1, x2 = xf[..., :half], xf[..., half:]
    out = jnp.concatenate([x1 * cos - x2 * sin, x2 * cos + x1 * sin], axis=-1)
    return out.astype(x.dtype)


def gla_chunked(q, k, v, g, strict):
    B, H, S, dk = q.shape
    C = GLA_CHUNK
    N = S // C

    def to_chunks(t):
        return t.reshape(B, H, N, C, t.shape[-1]).transpose(2, 0, 1, 3, 4)

    qc, kc, vc, gc = to_chunks(q), to_chunks(k), to_chunks(v), to_chunks(g)
    bc = jnp.cumsum(gc, axis=3)
    t_idx = jnp.arange(C)
    if strict:
        mask = t_idx[:, None] > t_idx[None, :]
    else:
        mask = t_idx[:, None] >= t_idx[None, :]

    def step(state, inp):
        qi, ki, vi, bi = inp
        diff = bi[:, :, :, None, :] - bi[:, :, None, :, :]
        decay = jnp.exp(jnp.where(mask[None, None, :, :, None], diff, -jnp.inf))
        scores = jnp.einsum('bhtd,bhsd,bhtsd->bhts', qi, ki, decay)
        o = jnp.einsum('bhts,bhsv->bhtv', scores, vi) + jnp.einsum('bhtd,bhdv->bhtv', qi * jnp.exp(bi), state)
        b_last = bi[:, :, -1:, :]
        k_dec = ki * jnp.exp(b_last - bi)
        state = state * jnp.exp(b_last[:, :, 0, :])[..., None] + jnp.einsum('bhsd,bhsv->bhdv', k_dec, vi)
        return state, o

    state0 = jnp.zeros((B, H, dk, v.shape[-1]), jnp.float32)
    _, oc = lax.scan(step, state0, (qc, kc, vc, bc))
    return oc.transpose(1, 2, 0, 3, 4).reshape(B, H, S, v.shape[-1])


def gla_branch(q_a, k_a, v_a, gr_f, gr_b, wg2_f, bg2_f, wg2_b, bg2_b, norm_g):
    B, S, _ = q_a.shape

    def heads(t, d):
        return t.reshape(B, S, GLA_HEADS, d).transpose(0, 2, 1, 3).astype(jnp.float32)

    q = heads(q_a, GLA_DK) * (GLA_DK ** -0.5)
    k = heads(k_a, GLA_DK)
    v = heads(v_a, GLA_DV)
    g_f = heads(jax.nn.log_sigmoid((gr_f @ wg2_f + bg2_f).astype(jnp.float32)) / GLA_GATE_NORM, GLA_DK)
    g_b = heads(jax.nn.log_sigmoid((gr_b @ wg2_b + bg2_b).astype(jnp.float32)) / GLA_GATE_NORM, GLA_DK)
    flip = lambda t: jnp.flip(t, axis=2)
    o_fwd = gla_chunked(q, k, v, g_f, strict=False)
    o_bwd = flip(gla_chunked(flip(q), flip(k), flip(v), flip(g_b), strict=True))
    o = rmsnorm(o_fwd + o_bwd, norm_g)
    return o.transpose(0, 2, 1, 3).reshape(B, S, GLA_V_W).astype(q_a.dtype)


def mla_branch(cq, ckv, k_r, q_norm_g, w_uq, kv_norm_g, w_ukv):
    B, S, _ = cq.shape
    pos = jnp.arange(S)
    c_q = rmsnorm(cq, q_norm_g)
    qf = (c_q @ w_uq).reshape(B, S, MLA_HEADS, MLA_NOPE + MLA_ROPE)
    q_nope = qf[..., :MLA_NOPE]
    q_pe = apply_rope(qf[..., MLA_NOPE:], pos)
    c_kv = rmsnorm(ckv, kv_norm_g)
    kv = (c_kv @ w_ukv).reshape(B, S, MLA_HEADS, MLA_NOPE + MLA_V)
    k_nope = kv[..., :MLA_NOPE]
    v = kv[..., MLA_NOPE:]
    k_pe = apply_rope(k_r[:, :, None, :], pos)[:, :, 0, :]
    scale = (MLA_NOPE + MLA_ROPE) ** -0.5
    NB = S // MLA_QBLOCK

    def blocks(t):
        return t.reshape(B, NB, MLA_QBLOCK, MLA_HEADS, t.shape[-1]).transpose(1, 0, 2, 3, 4)

    def attend(qb):
        qn, qp = qb
        s = (jnp.einsum('bqhd,bkhd->bhqk', qn, k_nope) + jnp.einsum('bqhr,bkr->bhqk', qp, k_pe)) * scale
        p = jax.nn.softmax(s.astype(jnp.float32), axis=-1).astype(v.dtype)
        return jnp.einsum('bhqk,bkhv->bqhv', p, v)

    ob = lax.map(attend, (blocks(q_nope), blocks(q_pe)))
    return ob.transpose(1, 0, 2, 3, 4).reshape(B, S, MLA_V_W)


def hybrid_layer(x, pre_g, w_in, wg2_f, bg2_f, wg2_b, bg2_b, gla_norm_g, q_norm_g, w_uq,
                 kv_norm_g, w_ukv, w_branch_a, w_branch_b, w_out, post_g):
    h = rmsnorm(x, pre_g)
    proj = h @ w_in
    (q_a, k_a, v_a, gr_f, gr_b, z_a, cq, ckv, k_r, z_b, gate_a, gate_b) = split_cols(proj, SPLIT_SIZES)
    o_a = gla_branch(q_a, k_a, v_a, gr_f, gr_b, wg2_f, bg2_f, wg2_b, bg2_b, gla_norm_g)
    o_b = mla_branch(cq, ckv, k_r, q_norm_g, w_uq, kv_norm_g, w_ukv)
    y_a = o_a * jax.nn.silu(z_a)
    y_b = o_b * jax.nn.silu(z_b)
    merged = jax.nn.sigmoid(gate_a) * (y_a @ w_branch_a) + jax.nn.sigmoid(gate_b) * (y_b @ w_branch_b)
    out = merged @ w_out
    return x + rmsnorm(out, post_g)


def setup_inputs(seed: int = 0) -> dict:
    key = jax.random.key(seed)
    ks = jax.random.split(key, 20)
    L = DEPTH
    nrm = lambda k, shape, fan: jax.random.normal(k, shape, jnp.float32) * (fan ** -0.5)
    gain = lambda k, shape: 1.0 + 0.05 * jax.random.normal(k, shape, jnp.float32)
    return {
        'x_prompt': jax.random.normal(ks[0], (BATCH, SEQ, D_MODEL), jnp.float32),
        'x_sample': jax.random.normal(ks[1], (DEC_BATCH, DEC_SEQ, D_MODEL), jnp.float32),
        'pre_norm_g': gain(ks[2], (L, D_MODEL)),
        'w_in': nrm(ks[3], (L, D_MODEL, D_IN), D_MODEL),
        'gla_wg2_fwd': nrm(ks[4], (L, GLA_GATE_RANK, GLA_QK_W), GLA_GATE_RANK),
        'gla_bg2_fwd': 0.1 * jax.random.normal(ks[5], (L, GLA_QK_W), jnp.float32),
        'gla_wg2_bwd': nrm(ks[6], (L, GLA_GATE_RANK, GLA_QK_W), GLA_GATE_RANK),
        'gla_bg2_bwd': 0.1 * jax.random.normal(ks[7], (L, GLA_QK_W), jnp.float32),
        'gla_out_norm_g': gain(ks[8], (L, GLA_DV)),
        'mla_q_norm_g': gain(ks[9], (L, MLA_Q_RANK)),
        'mla_w_uq': nrm(ks[10], (L, MLA_Q_RANK, MLA_HEADS * (MLA_NOPE + MLA_ROPE)), MLA_Q_RANK),
        'mla_kv_norm_g': gain(ks[11], (L, MLA_KV_RANK)),
        'mla_w_ukv': nrm(ks[12], (L, MLA_KV_RANK, MLA_HEADS * (MLA_NOPE + MLA_V)), MLA_KV_RANK),
        'w_branch_a': nrm(ks[13], (L, GLA_V_W, D_MODEL), GLA_V_W),
        'w_branch_b': nrm(ks[14], (L, MLA_V_W, D_MODEL), MLA_V_W),
        'w_out': nrm(ks[15], (L, D_MODEL, D_MODEL), D_MODEL),
        'post_norm_g': gain(ks[16], (L, D_MODEL)),
    }


def reference(x_prompt, x_sample, pre_norm_g, w_in, gla_wg2_fwd, gla_bg2_fwd, gla_wg2_bwd, gla_bg2_bwd,
              gla_out_norm_g, mla_q_norm_g, mla_w_uq, mla_kv_norm_g, mla_w_ukv, w_branch_a, w_branch_b,
              w_out, post_norm_g):
    y_prompt = x_prompt
    y_sample = x_sample
    for l in range(DEPTH):
        p = (pre_norm_g[l], w_in[l], gla_wg2_fwd[l], gla_bg2_fwd[l], gla_wg2_bwd[l], gla_bg2_bwd[l],
             gla_out_norm_g[l], mla_q_norm_g[l], mla_w_uq[l], mla_kv_norm_g[l], mla_w_ukv[l],
             w_branch_a[l], w_branch_b[l], w_out[l], post_norm_g[l])
        y_prompt = hybrid_layer(y_prompt, *p)
        y_sample = hybrid_layer(y_sample, *p)
    return (y_prompt, y_sample)
```

```python
import contextlib
import numpy as np
import ml_dtypes
import concourse.bass as bass
import concourse.mybir as mybir
from concourse.bass_utils import run_bass_kernel_spmd

F32 = mybir.dt.float32
BF16 = mybir.dt.bfloat16
AF = mybir.ActivationFunctionType
ALU = mybir.AluOpType

ENGS = ("pe", "act", "dve", "pool", "sp")
NDMA_SEM = 8


class Buf:
    __slots__ = ("name", "w", "rc", "rd")

    def __init__(self, name=""):
        self.name = name
        self.w = None
        self.rc = {}
        self.rd = []

    def events(self):
        ev = set()
        if self.w is not None:
            ev.add(self.w)
        for e, i in self.rc.items():
            ev.add(("c", e, i))
        ev.update(self.rd)
        return ev


def alias_switch(old, new):
    ev = set()
    for b in old:
        ev |= b.events()
    for b in new:
        for e in ev:
            if e[0] == "c":
                if b.rc.get(e[1], -1) < e[2]:
                    b.rc[e[1]] = e[2]
            elif e not in b.rd:
                b.rd.append(e)


class Sched:
    def __init__(self, nc):
        self.nc = nc
        self.ops = {e: [] for e in ENGS}
        self.ndma = {e: 0 for e in ENGS}

    def op(self, eng, fn, reads=(), writes=(), dma=False):
        lst = self.ops[eng]
        idx = len(lst)
        deps = set()
        raw = set()
        for b in reads:
            if b.w is not None:
                deps.add(b.w)
                raw.add(b.w)
        for b in writes:
            deps |= b.events()
        if dma:
            k = self.ndma[eng]
            self.ndma[eng] += 1
            ev = ("d", eng, k)
            if k >= NDMA_SEM:
                deps.add(("d", eng, k - NDMA_SEM))
        else:
            ev = ("c", eng, idx)
        pruned = set()
        for d in deps:
            if d[0] == "c" and d[1] == eng:
                if eng == "pe":
                    continue
            pruned.add(d)
        for b in writes:
            b.w = ev
            b.rc = {}
            b.rd = []
        for b in reads:
            if b.w == ev:
                continue
            if dma:
                b.rd.append(ev)
            else:
                b.rc[eng] = idx
        lst.append({"fn": fn, "deps": pruned, "dma": dma, "ev": ev, "sig": False})
        return ev

    def emit(self, final_events):
        nc = self.nc
        ops = self.ops
        for e in ENGS:
            for o in ops[e]:
                for d in o["deps"]:
                    if d[0] == "c":
                        ops[d[1]][d[2]]["sig"] = True
        for d in final_events:
            if d[0] == "c":
                ops[d[1]][d[2]]["sig"] = True
        semval = {}
        for e in ENGS:
            c = 0
            for i, o in enumerate(ops[e]):
                if o["dma"]:
                    continue
                if o["sig"]:
                    c += 1
                    semval[(e, i)] = c
        with contextlib.ExitStack() as st:
            csem = {e: st.enter_context(nc.semaphore("c_" + e)) for e in ENGS}
            dsem = {e: [st.enter_context(nc.semaphore("d_%s_%d" % (e, j))) for j in range(NDMA_SEM)]
                    for e in ENGS if self.ndma[e] > 0}
            block = st.enter_context(nc.Block())

            def resolve(d):
                if d[0] == "c":
                    return csem[d[1]], semval[(d[1], d[2])]
                k = d[2]
                return dsem[d[1]][k % NDMA_SEM], 16 * (k // NDMA_SEM + 1)

            def run(eng_name, eng):
                seen = {}
                for o in ops[eng_name]:
                    need = {}
                    for d in o["deps"]:
                        s, v = resolve(d)
                        key = id(s)
                        if seen.get(key, 0) >= v:
                            continue
                        if key not in need or need[key][1] < v:
                            need[key] = (s, v)
                    for key, (s, v) in need.items():
                        eng.wait_ge(s, v)
                        seen[key] = v
                    ins = o["fn"](eng)
                    if o["dma"]:
                        k = o["ev"][2]
                        ins.then_inc(dsem[eng_name][k % NDMA_SEM], 16)
                    elif o["sig"]:
                        ins.then_inc(csem[eng_name], 1)
                if eng_name == "sp":
                    for d in final_events:
                        s, v = resolve(d)
                        eng.wait_ge(s, v)

            @block.tensor
            def _(eng):
                run("pe", eng)

            @block.scalar
            def _(eng):
                run("act", eng)

            @block.vector
            def _(eng):
                run("dve", eng)

            @block.gpsimd
            def _(eng):
                run("pool", eng)

            @block.sync
            def _(eng):
                run("sp", eng)


D = 1024
DIN = 6624
C_Q, C_K, C_V, C_GF, C_GB, C_ZA, C_CQ, C_CKV, C_KR, C_ZB, C_GA, C_GBT = (
    0, 512, 1024, 2048, 2064, 2080, 3104, 3360, 3488, 3552, 4576, 5600)
EPS = 1e-6
NCORE = 8
FULL_CFG = dict(NP=2, SP=4096, SS=16384, TB=512)


class TT:
    __slots__ = ("t", "b")

    def __init__(self, t, b):
        self.t = t
        self.b = b


def build(cfg):
    NP, SP, SS, TB = cfg["NP"], cfg["SP"], cfg["SS"], cfg["TB"]
    NT = TB // 128
    SL = SS // NCORE
    NBP = SP // TB
    NBS = SL // TB
    NBC = (SS - SL) // TB
    CKT = 4
    nc = bass.Bass("TRN2", target_bir_lowering=False)

    def din(name, shape, dt=F32):
        return nc.dram_tensor(name, list(shape), dt, kind="ExternalInput").ap()

    xp = din("xp", [NP * SP, D])
    xs = din("xs", [SS, D])
    w_in = din("w_in", [D, DIN])
    wg2 = din("wg2", [2, 17, 512])
    w_uq = din("w_uq", [256, 1536])
    w_ukv = din("w_ukv", [128, 2048])
    w_a = din("w_a", [D, D])
    w_b = din("w_b", [D, D])
    w_o = din("w_o", [D, D])
    pre_g = din("pre_g", [128, 8])
    vecs = din("vecs", [1024 + 1024 + 256 + 128])
    ropep = din("ropep", [SP // 128, 128, 64])
    ropes = din("ropes", [SS // 128, 128, 64])
    cst = din("cst", [128, 6 * 128])
    cmask = din("cmask", [128, 2 * max(NBC, 1)])
    yp = nc.dram_tensor("yp", [NP * SP, D], F32, kind="ExternalOutput").ap()
    ys = nc.dram_tensor("ys", [SL, D], F32, kind="ExternalOutput").ap()

    wsc = nc.dram_tensor("wsc", [D, DIN], BF16).ap()
    wabo = nc.dram_tensor("wabo", [3, D, D], BF16).ap()
    SMAX = max(SP, SS)
    ctx_kl = nc.dram_tensor("ctx_kl", [128, SMAX], BF16).ap()
    ctx_kp = nc.dram_tensor("ctx_kp", [128, 2, SMAX], BF16).ap()
    ctx_v = nc.dram_tensor("ctx_v", [128, SMAX // 128, 128], BF16).ap()
    bnd = nc.dram_tensor("bnd", [max(NBP, NBS), 128, 1024], BF16).ap()
    NBM = max(NBP, NBS)
    kv_k = nc.dram_tensor("kv_k", [NBM, 128, NT * 512], F32).ap()
    kv_v = nc.dram_tensor("kv_v", [NBM, 128, NT * 1024], BF16).ap()
    kv_g = nc.dram_tensor("kv_g", [NBM, 16, 2 * TB], BF16).ap()
    wsc_v = wsc.rearrange("(dt p) n -> p dt n", p=128)
    wabo_v = wabo.rearrange("w (dt p) n -> w p dt n", p=128)

    S = Sched(nc)
    st = contextlib.ExitStack()
    cnt = [0]

    def sb(shape, dt, name=None):
        cnt[0] += 1
        nm = "sb_" + (name or ("t%d" % cnt[0]))
        t = st.enter_context(nc.sbuf_tensor(nm, list(shape), dt))
        return TT(t, Buf(nm))

    def ps(shape, dt, name):
        t = st.enter_context(nc.psum_tensor(name, list(shape), dt))
        return t

    with st:
        pp = [ps([128, 2, 512], F32, "pp%d" % i) for i in range(3)]
        PB = []
        for i in range(3):
            for j in range(2):
                PB.append(TT(pp[i][:, j, :], Buf("pb%d%d" % (i, j))))
        pq = ps([128, 512], F32, "pq")
        PB.append(TT(pq[:, :], Buf("pq")))
        pq2 = ps([128, 512], F32, "pq2")
        PB.append(TT(pq2[:, :], Buf("pq2")))
        rot = {"g": 0, "t": 0}

        def bank():
            b = PB[rot["g"] % 8]
            rot["g"] += 1
            return b

        def tbank():
            b = bank()
            return TT(b.t.bitcast(BF16), b.b)

        cst_t = sb([128, 6 * 128], F32, "cst")
        ident_f = cst_t.t[:, 0:128]
        tri = [cst_t.t[:, 128:256], cst_t.t[:, 256:384]]
        ones_f = cst_t.t[:, 640:768]
        ident_b = sb([128, 128], BF16, "identb")
        maskb = sb([128, 2, 128], BF16, "maskb")
        ones16 = sb([128, 1], F32, "ones16")
        pre_g_t = sb([128, 8], F32, "preg")
        vec_t = sb([128, 2432], F32, "vecs")
        post_g_bc = vec_t.t[:, 0:1024]
        gla_g_bc = vec_t.t[:, 1024:2048]
        q_g_bc = vec_t.t[:, 2048:2304]
        kv_g_bc = vec_t.t[:, 2304:2432]
        cmask_t = sb([128, 2 * max(NBC, 1)], F32, "cmask")
        wg2_t = sb([128, 2, 512], BF16, "wg2")
        wgr_t = sb([128, 8, 32], BF16, "wgr")
        wcomb = sb([128, 2, 8, 128], BF16, "wcomb")
        wpe = sb([128, 2, 8, 64], BF16, "wpe")
        wuv = sb([128, 8, 128], BF16, "wuv")
        Sst = [sb([128, 1024], F32, "Sf"), sb([128, 1024], F32, "Sb")]
        Pdec = sb([128, 4], F32, "Pdec")
        xsl = [sb([128, 1024], F32, "x%d" % i) for i in range(2)]
        xnsl = [sb([128, 1024], BF16, "xn%d" % i) for i in range(2)]
        ssq = sb([128, 8], F32, "ssq")
        rstd = sb([128, 8], F32, "rstd")
        hT = sb([128, 8, TB], BF16, "hT")
        WS = [sb([128, 8, 512], BF16, "ws%d" % i) for i in range(3)]
        CX = [(sb([128, CKT * 128], BF16, "cxl%d" % i), sb([128, 2, CKT * 128], BF16, "cxp%d" % i),
               sb([128, CKT, 128], BF16, "cxv%d" % i)) for i in range(3)]
        ropet = sb([128, NT, 64], F32, "ropet")
        qkraw = sb([128, 8, TB], F32, "qkraw")
        qkt = sb([128, 4, 4, TB], BF16, "qkt")
        kraw = sb([128, NT, 512], F32, "kraw")
        vtok = sb([128, NT, 1024], BF16, "vtok")
        sbf = sb([128, 2, NT, 1024], BF16, "sbf")
        graug = sb([128, 2, TB], BF16, "graug")
        e1 = sb([128, 512], F32, "e1")
        junk = TT(e1.t.bitcast(BF16), e1.b)
        e1s = [e1, sb([128, 512], F32, "e1b")]
        gp = [sb([128, 512], F32, "gp%d" % i) for i in range(2)]
        eposs = [sb([128, 512], F32, "epos%d" % i) for i in range(2)]
        etps = [sb([128, 4, 128], F32, "etp%d" % i) for i in range(2)]
        etns = [sb([128, 4, 128], F32, "etn%d" % i) for i in range(2)]
        dcy = sb([128, 2, 4], F32, "dcy")
        ktl = [sb([128, 512], BF16, "ktl%d" % i) for i in range(2)]
        atT = [sb([128, 4, 128], BF16, "at%d" % i) for i in range(2)]
        zg = sb([128, 1024], F32, "zg")
        tmpS = sb([128, 1024], F32, "tmpS")
        ya = [sb([128, 1024], BF16, "ya%d" % i) for i in range(2)]
        ss4 = sb([128, 8], F32, "ss4")
        rs4 = sb([128, 8], F32, "rs4")
        yaT = sb([128, 8, TB], BF16, "yaT")
        ybT = sb([128, 8, TB], BF16, "ybT")
        qk_ = qkt.t.rearrange("p a h t -> p (a h t)")
        ckvn = TT(qk_[:, 0:512].rearrange("p (a n) -> p a n", a=NT), qkt.b)
        kpe2 = TT(qk_[:, 512:1536].rearrange("p (a n) -> p a n", a=NT), qkt.b)
        klT = TT(qk_[:, 1536:2048], qkt.b)
        kpT = TT(qk_[:, 2048:3072].rearrange("p (v t) -> p v t", v=2), qkt.b)
        rtmp = [TT(tmpS.t[:, j * 256:(j + 1) * 256].rearrange("p (h n) -> p h n", h=8), tmpS.b) for j in range(4)]
        qf = qkraw.t.rearrange("p a t -> p (a t)")
        rinv = TT(qf[:, 0:512], Buf("rinv"))
        t1 = TT(qf[:, 512:1024], Buf("t1"))
        t2 = TT(qf[:, 1024:1536], Buf("t2"))
        yo = [TT(zg.t[:, :], zg.b), TT(tmpS.t[:, :], tmpS.b)]
        szs = [TT(qf[:, 3584 + 0:3584 + 512], Buf("sz0"))]
        sf_ = sbf.t.rearrange("p a c n -> p (a c n)")
        o_ = 0
        QT = TT(sf_[:, 0:8 * TB].rearrange("p (h t) -> p h t", h=8), Buf("QT"))
        mrgT = TT(QT.t, QT.b)
        o_ = 8 * TB
        qpeT = TT(sf_[:, o_:o_ + 4 * TB].rearrange("p (h t) -> p h t", h=4), Buf("qpeT"))
        o_ += 4 * TB
        pTs = [TT(sf_[:, o_ + i * 512:o_ + (i + 1) * 512], Buf("pT%d" % i)) for i in range(4)]
        kf_ = kraw.t.rearrange("p a n -> p (a n)")
        szs.append(TT(kf_[:, 0:512], Buf("sz1")))
        cqs = TT(kf_[:, 512:1024].rearrange("p (a n) -> p a n", a=2), Buf("cqs"))
        vf_ = vtok.t.rearrange("p a n -> p (a n)")
        cqT = TT(vf_[:, 0:2 * TB].rearrange("p (c t) -> p c t", c=2), Buf("cqT"))
        olat = [TT(vf_[:, 2 * TB + i * 512:2 * TB + (i + 1) * 512], Buf("olat%d" % i)) for i in range(2)]
        cqn = TT(vf_[:, 2 * TB + 1024:2 * TB + 1024 + NT * 256].rearrange("p (a n) -> p a n", a=NT), Buf("cqn"))
        qpe_tok = TT(qk_[:, 0:NT * 512].rearrange("p (a n) -> p a n", a=NT), Buf("qpetok"))
        racc_f = qk_[:, 4096:8192].bitcast(F32)
        raccs = [[TT(racc_f[:, (hh * 2 + pr) * 512:(hh * 2 + pr + 1) * 512], Buf("racc%d%d" % (hh, pr))) for pr in range(2)] for hh in range(2)]
        ALIAS = [
            (qkraw.b, [rinv.b, t1.b, t2.b, szs[0].b]),
            (sbf.b, [QT.b, qpeT.b] + [p.b for p in pTs]),
            (kraw.b, [szs[1].b, cqs.b]),
            (vtok.b, [cqT.b, olat[0].b, olat[1].b, cqn.b]),
            (qkt.b, [qpe_tok.b, raccs[0][0].b, raccs[0][1].b, raccs[1][0].b, raccs[1][1].b]),
        ]

        def to_G():
            for host, views in ALIAS:
                alias_switch(views, [host])

        def to_M():
            for host, views in ALIAS:
                alias_switch([host], views)

        def MM(out, lhsT, rhs, start, stop, reads, writes):
            S.op("pe", lambda e: e.matmul(out, lhsT=lhsT, rhs=rhs, start=start, stop=stop), reads, writes)

        def TR(out, in_, ident, reads, writes):
            S.op("pe", lambda e: e.transpose(out=out, in_=in_, identity=ident), reads, writes)

        def ACT(out, in_, func, reads, writes, **kw):
            S.op("act", lambda e: e.activation(out=out, in_=in_, func=func, **kw), reads, writes)

        def TTOP(eng, out, in0, in1, op, reads, writes):
            S.op(eng, lambda e: e.tensor_tensor(out=out, in0=in0, in1=in1, op=op), reads, writes)

        def TS(eng, out, in0, s1, s2, op0, op1, reads, writes):
            if s2 is None:
                S.op(eng, lambda e: e.tensor_scalar(out=out, in0=in0, scalar1=s1, scalar2=None, op0=op0), reads, writes)
            else:
                S.op(eng, lambda e: e.tensor_scalar(out=out, in0=in0, scalar1=s1, scalar2=s2, op0=op0, op1=op1), reads, writes)

        def STT(eng, out, in0, scalar, in1, op0, op1, reads, writes):
            S.op(eng, lambda e: e.scalar_tensor_tensor(out=out, in0=in0, scalar=scalar, in1=in1, op0=op0, op1=op1), reads, writes)

        def CP(eng, out, in_, reads, writes):
            if eng == "act":
                S.op("act", lambda e: e.copy(out=out, in_=in_), reads, writes)
            else:
                S.op(eng, lambda e: e.tensor_copy(out=out, in_=in_), reads, writes)

        def DMA(eng, out, in_, reads, writes):
            return S.op(eng, lambda e: e.dma_start(out=out, in_=in_), reads, writes, dma=True)

        def rstd_from_ss(ss_ap, out_ap, n, reads_b, out_b):
            TS("dve", out_ap, ss_ap, 1.0 / n, EPS, ALU.mult, ALU.add, [reads_b], [out_b])
            ACT(out_ap, out_ap, AF.Ln, [out_b], [out_b])
            ACT(out_ap, out_ap, AF.Exp, [out_b], [out_b], scale=-0.5)

        wsc_bs = [Buf("wsc%d" % i) for i in range(8)]
        wabo_bs = [Buf("wabo%d" % i) for i in range(8)]
        ctx_b = Buf("ctx")
        bnd_b = [Buf("bnd%d" % i) for i in range(max(NBP, NBS))]
        kvsc_b = [Buf("kvsc%d" % i) for i in range(max(NBP, NBS))]
        out_events = []

        DMA("sp", cst_t.t[:, :], cst[:, :], [], [cst_t.b])
        DMA("sp", pre_g_t.t[:, :], pre_g[:, :], [], [pre_g_t.b])
        DMA("sp", vec_t.t[:, :], vecs.partition_broadcast(128), [], [vec_t.b])
        DMA("sp", cmask_t.t[:, :], cmask[:, :], [], [cmask_t.b])
        CP("dve", ident_b.t[:, :], ident_f, [cst_t.b], [ident_b.b])
        CP("dve", maskb.t[:, :, :], cst_t.t[:, 384:640].rearrange("p (a n) -> p a n", a=2), [cst_t.b], [maskb.b])
        S.op("pool", lambda e: e.memset(ones16.t[:, :], 1.0 / 16.0), [], [ones16.b])
        S.op("pool", lambda e: e.memset(graug.t[:, :, :], 1.0), [], [graug.b])
        k = 0
        conv_eng = ["dve", "act", "pool"]
        stg_f = [xsl[0], xsl[1]] + [TT(qf[:, i * 1024:(i + 1) * 1024], Buf("stgf%d" % i)) for i in range(4)]
        stg_b = [xnsl[0], xnsl[1]] + [TT(sf_[:, i * 1024:(i + 1) * 1024], Buf("stgb%d" % i)) for i in range(4)]

        def convert(src_ap, dst_ap, dstb, ncol):
            nonlocal k
            xt = stg_f[k % 6]
            xn_ = stg_b[k % 6]
            DMA("sp", xt.t[:, 0:ncol], src_ap, [], [xt.b])
            CP(conv_eng[k % 3], xn_.t[:, 0:ncol], xt.t[:, 0:ncol], [xt.b], [xn_.b])
            DMA("pool", dst_ap, xn_.t[:, 0:ncol], [xn_.b], [dstb])
            k += 1

        for c0 in range(0, DIN, 1024):
            ncol = min(1024, DIN - c0)
            for dt in range(8):
                convert(w_in[dt * 128:(dt + 1) * 128, c0:c0 + ncol], wsc[dt * 128:(dt + 1) * 128, c0:c0 + ncol], wsc_bs[dt], ncol)
        for wi, wsrc in enumerate((w_a, w_b, w_o)):
            for dt in range(8):
                convert(wsrc[dt * 128:(dt + 1) * 128, :], wabo[wi, dt * 128:(dt + 1) * 128, :], wabo_bs[dt], 1024)
        alias_switch([t.b for t in stg_f[2:]], [qkraw.b])
        alias_switch([t.b for t in stg_b[2:]], [sbf.b])
        DMA("sp", wgr_t.t[:, :, :], wsc_v[:, :, C_GF:C_GF + 32], wsc_bs, [wgr_t.b])
        wg2f = TT(tmpS.t[0:17, :].rearrange("p (a n) -> p a n", a=2), tmpS.b)
        DMA("sp", wg2f.t[:, :, :], wg2.rearrange("a r n -> r a n"), [], [wg2f.b])
        S.op("pool", lambda e: e.memset(wg2_t.t[:, :, :], 0.0), [], [wg2_t.b])
        CP("dve", wg2_t.t[0:17, :, :], wg2f.t[:, :, :], [wg2f.b, wg2_t.b], [wg2_t.b])
        uqf = qkraw.t.rearrange("p a t -> p (a t)")[:, 0:3072].rearrange("p (c n) -> p c n", c=2)
        ukf = kraw.t.rearrange("p a n -> p (a n)")[:, 0:2048]
        DMA("sp", uqf, w_uq.rearrange("(c p) n -> p c n", p=128), [], [qkraw.b])
        DMA("sp", ukf, w_ukv[:, :], [], [kraw.b])
        uq4 = uqf.rearrange("p c (h n) -> p c h n", h=8)
        uk3 = ukf.rearrange("p (h n) -> p h n", h=8)
        CP("dve", wpe.t[:, :, :, :], uq4[:, :, :, 128:192], [qkraw.b], [wpe.b])
        CP("dve", wuv.t[:, :, :], uk3[:, :, 128:256], [kraw.b], [wuv.b])
        wT = TT(zg.t[:, 0:384].rearrange("p (a n) -> p a n", a=3), zg.b)
        for h in range(8):
            pb = bank()
            for c in range(2):
                TR(pb.t[:, c * 128:(c + 1) * 128], uq4[:, c, h, 0:128], ident_f, [qkraw.b, cst_t.b], [pb.b])
            TR(pb.t[:, 256:384], uk3[:, h, 0:128], ident_f, [kraw.b, cst_t.b], [pb.b])
            CP("dve", wT.t[:, :, :], pb.t[:, 0:384].rearrange("p (a n) -> p a n", a=3), [pb.b], [wT.b])
            pc = bank()
            for c in range(2):
                MM(pc.t[:, c * 128:(c + 1) * 128], wT.t[:, c, :], wT.t[:, 2, :], True, True, [wT.b], [pc.b])
            CP("dve", wcomb.t[:, :, h, :], pc.t[:, 0:256].rearrange("p (c n) -> p c n", c=2), [pc.b], [wcomb.b])

        wq = {"n": 0}

        def wload(view, c0, ncol, srcb):
            s_ = WS[wq["n"] % 3]
            wq["n"] += 1
            DMA("sp", s_.t[:, :, 0:ncol], view[:, :, c0:c0 + ncol], srcb, [s_.b])
            return s_

        def wload2(viewA, cA, viewB, cB, srcA, srcB):
            s_ = WS[wq["n"] % 3]
            wq["n"] += 1
            DMA("sp", s_.t[:, :, 0:256], viewA[:, :, cA:cA + 256], srcA, [s_.b])
            DMA("sp", s_.t[:, :, 256:512], viewB[:, :, cB:cB + 256], srcB, [s_.b])
            return s_

        xq = {"n": 0}

        def xload_pair(xsrc, row0, i0):
            slots = []
            for i in (i0, i0 + 1):
                xt = xsl[xq["n"] % 2]
                xn_ = xnsl[xq["n"] % 2]
                xq["n"] += 1
                slots.append((i, xt, xn_))
                DMA("sp", xt.t[:, :], xsrc[row0 + i * 128:row0 + (i + 1) * 128, :], [], [xt.b])
            return slots

        def hprep(xsrc, row0, pre_slots=None):
            for i0 in range(0, NT, 2):
                if i0 == 0 and pre_slots is not None:
                    slots = pre_slots
                else:
                    slots = xload_pair(xsrc, row0, i0)
                for (i, xt, xn_) in slots:
                    ACT(junk.t[:, :], xt.t[:, :], AF.Square, [xt.b], [junk.b, ssq.b], accum_out=ssq.t[:, i:i + 1])
                rstd_from_ss(ssq.t[:, i0:i0 + 2], rstd.t[:, i0:i0 + 2], D, ssq.b, rstd.b)
                for (i, xt, xn_) in slots:
                    ACT(xn_.t[:, :], xt.t[:, :], AF.Copy, [xt.b, rstd.b], [xn_.b], scale=rstd.t[:, i:i + 1])
                    for half in range(2):
                        tb_ = tbank()
                        for j in range(4):
                            dt = half * 4 + j
                            TR(tb_.t[:, j * 128:(j + 1) * 128], xn_.t[:, dt * 128:(dt + 1) * 128], ident_b.t[:, :], [xn_.b, ident_b.b], [tb_.b])
                        TTOP("dve", hT.t[:, half * 4:half * 4 + 4, i * 128:(i + 1) * 128],
                             tb_.t[:, 0:512].rearrange("p (a n) -> p a n", a=4),
                             pre_g_t.t[:, half * 4:half * 4 + 4].unsqueeze(2).to_broadcast([128, 4, 128]),
                             ALU.mult, [tb_.b, pre_g_t.b], [hT.b])

        def proj_tok(wslot, ncol, dst_fn):
            for i in range(NT):
                pb = bank()
                for dt in range(8):
                    MM(pb.t[:, 0:ncol], hT.t[:, dt, i * 128:(i + 1) * 128], wslot.t[:, dt, 0:ncol], dt == 0, dt == 7,
                       [hT.b, wslot.b], [pb.b])
                dst_fn(i, pb)

        def proj_feat(wslot, col, m, dst_fn, arg):
            pb = bank()
            for dt in range(8):
                MM(pb.t[0:m, 0:TB], wslot.t[:, dt, col:col + m], hT.t[:, dt, :], dt == 0, dt == 7, [hT.b, wslot.b], [pb.b])
            dst_fn(arg, pb)

        def gates_multi(items, need_T):
            pbs, pcs, pxs = {}, {}, {}
            for (i, dirn, sl_) in items:
                pb = bank()
                pbs[sl_] = pb
                MM(pb.t[:, :], graug.t[:, dirn, i * 128:(i + 1) * 128], wg2_t.t[:, dirn, :], True, True, [graug.b, wg2_t.b], [pb.b])
            for (i, dirn, sl_) in items:
                ACT(e1s[sl_].t[:, :], pbs[sl_].t[:, :], AF.Exp, [pbs[sl_].b], [e1s[sl_].b], scale=-1.0)
            for (i, dirn, sl_) in items:
                ACT(gp[sl_].t[:, :], e1s[sl_].t[:, :], AF.Ln, [e1s[sl_].b], [gp[sl_].b], bias=1.0)
            for (i, dirn, sl_) in items:
                g_ = gp[sl_]
                pc = bank()
                pcs[sl_] = pc
                MM(pc.t[:, :], tri[dirn], g_.t[:, :], True, True, [cst_t.b, g_.b], [pc.b])
                px = bank()
                pxs[sl_] = px
                if need_T:
                    for h in range(4):
                        MM(px.t[:, h * 128:(h + 1) * 128], g_.t[:, h * 128:(h + 1) * 128], tri[dirn], True, True, [g_.b, cst_t.b], [px.b])
                else:
                    for h in range(4):
                        MM(px.t[:, h:h + 1], g_.t[:, h * 128:(h + 1) * 128], ones16.t[:, 0:1], True, True, [g_.b, ones16.b], [px.b])
            for (i, dirn, sl_) in items:
                ACT(eposs[sl_].t[:, :], pcs[sl_].t[:, :], AF.Exp, [pcs[sl_].b], [eposs[sl_].b])
                px = pxs[sl_]
                if need_T:
                    ACT(etps[sl_].t[:, :, :], px.t[:, :].rearrange("p (h n) -> p h n", h=4), AF.Exp, [px.b], [etps[sl_].b])
                    ACT(etns[sl_].t[:, :, :], px.t[:, :].rearrange("p (h n) -> p h n", h=4), AF.Exp, [px.b], [etns[sl_].b], scale=-1.0)
                    col = 127 if dirn == 0 else 0
                    CP("dve", dcy.t[:, sl_, :], etns[sl_].t[:, :, col], [etns[sl_].b], [dcy.b])
                else:
                    ACT(dcy.t[:, sl_, :], px.t[:, 0:4], AF.Exp, [px.b], [dcy.b], scale=-1.0)

        def ktilde_U(i, sl_):
            kt_ = ktl[sl_]
            TTOP("dve", kt_.t[:, :], kraw.t[:, i, :], eposs[sl_].t[:, :], ALU.mult, [kraw.b, eposs[sl_].b], [kt_.b])
            ub = [bank(), bank()]
            for h in range(4):
                MM(ub[h // 2].t[:, (h % 2) * 256:(h % 2) * 256 + 256], kt_.t[:, h * 128:(h + 1) * 128],
                   vtok.t[:, i, h * 256:(h + 1) * 256], True, True, [kt_.b, vtok.b], [ub[h // 2].b])
            return ub

        def state_step(ub, dirn, mask_ap=None, sl_=None):
            Sx = Sst[dirn]
            if sl_ is None:
                sl_ = dirn
            for hh in range(2):
                TTOP("dve", tmpS.t[:, hh * 512:(hh + 1) * 512], ub[hh].t[:, :], Sx.t[:, hh * 512:(hh + 1) * 512], ALU.add,
                     [ub[hh].b, Sx.b], [tmpS.b])
            TTOP("dve", Sx.t[:, :].rearrange("p (h n) -> p h n", h=4), tmpS.t[:, :].rearrange("p (h n) -> p h n", h=4),
                 dcy.t[:, sl_, :].unsqueeze(2).to_broadcast([128, 4, 256]), ALU.mult, [tmpS.b, dcy.b], [Sx.b])
            if mask_ap is not None:
                TS("dve", Sx.t[:, :], Sx.t[:, :], mask_ap, None, ALU.mult, None, [Sx.b, cmask_t.b], [Sx.b])

        def gr_proj():
            for dirn in range(2):
                pb = bank()
                for dt in range(8):
                    MM(pb.t[0:16, 0:TB], wgr_t.t[:, dt, dirn * 16:(dirn + 1) * 16], hT.t[:, dt, :], dt == 0, dt == 7, [hT.b, wgr_t.b], [pb.b])
                CP("act", graug.t[0:16, dirn, :], pb.t[0:16, 0:TB], [pb.b], [graug.b])

        def kv_proj():
            wk = wload(wsc_v, C_K, 512, wsc_bs)
            proj_tok(wk, 512, lambda i, pb: CP("act", kraw.t[:, i, :], pb.t[:, :], [pb.b], [kraw.b]))
            for hv in range(2):
                wv = wload(wsc_v, C_V + hv * 512, 512, wsc_bs)
                proj_tok(wv, 512, lambda i, pb, hv=hv: CP("dve", vtok.t[:, i, hv * 512:(hv + 1) * 512], pb.t[:, :], [pb.b], [vtok.b]))

        def passA_block(xsrc, row0, rope_src, tile0, ctx_col0, store_bnd, fwd_acc, mask_cols, pre=False, next_hprep=None, next_xload=None):
            to_G()
            if not pre:
                hprep(xsrc, row0)
            S.op("pool", lambda e: e.memset(kpe2.t[:, :, 64:192], 0.0), [], [kpe2.b])
            if store_bnd is not None:
                CP("act", ya[0].t[:, :], Sst[1].t[:, :], [Sst[1].b], [ya[0].b])
                DMA("pool", bnd[store_bnd, :, :], ya[0].t[:, :], [ya[0].b], [bnd_b[store_bnd]])
            gr_proj()
            kv_proj()
            if next_xload is not None:
                next_xload()
            if store_bnd is not None:
                DMA("pool", kv_k[store_bnd, :, :], kraw.t[:, :, :].rearrange("p a n -> p (a n)"), [kraw.b], [kvsc_b[store_bnd]])
                DMA("pool", kv_v[store_bnd, :, :], vtok.t[:, :, :].rearrange("p a n -> p (a n)"), [vtok.b], [kvsc_b[store_bnd]])
                DMA("pool", kv_g[store_bnd, :, :], graug.t[0:16, :, :].rearrange("p a n -> p (a n)"), [graug.b], [kvsc_b[store_bnd]])
            wc = wload(wsc_v, C_CKV, 192, wsc_bs)
            DMA("sp", ropet.t[:, :, :], rope_src[tile0:tile0 + NT, :, :].rearrange("a p n -> p a n"), [], [ropet.b])

            def ctx_evac(i, pb):
                ACT(junk.t[:, 0:128], pb.t[:, 0:128], AF.Square, [pb.b], [junk.b, ss4.b], accum_out=ss4.t[:, i:i + 1])
                rstd_from_ss(ss4.t[:, i:i + 1], rs4.t[:, i:i + 1], 128, ss4.b, rs4.b)
                STT("dve", ckvn.t[:, i, :], pb.t[:, 0:128], rs4.t[:, i:i + 1], kv_g_bc, ALU.mult, ALU.mult, [pb.b, rs4.b, vec_t.b], [ckvn.b])
                x1 = pb.t[:, 128:160]
                x2 = pb.t[:, 160:192]
                cs = ropet.t[:, i, 0:32]
                sn = ropet.t[:, i, 32:64]
                r = [rtmp[j].t[:, 0, :] for j in range(4)]
                rb = [rtmp[j].b for j in range(4)]
                TTOP("dve", r[0], x1, cs, ALU.mult, [pb.b, ropet.b], [rb[0]])
                TTOP("dve", r[1], x2, sn, ALU.mult, [pb.b, ropet.b], [rb[1]])
                TTOP("dve", r[2], x2, cs, ALU.mult, [pb.b, ropet.b], [rb[2]])
                TTOP("dve", r[3], x1, sn, ALU.mult, [pb.b, ropet.b], [rb[3]])
                TTOP("dve", kpe2.t[:, i, 0:32], r[0], r[1], ALU.subtract, [rb[0], rb[1]], [kpe2.b])
                TTOP("dve", kpe2.t[:, i, 32:64], r[2], r[3], ALU.add, [rb[2], rb[3]], [kpe2.b])
                CP("dve", kpe2.t[:, i, 192:256], kpe2.t[:, i, 0:64], [kpe2.b], [kpe2.b])

            proj_tok(wc, 192, ctx_evac)
            if next_hprep is not None:
                next_hprep()
            for i in range(NT):
                tb_ = tbank()
                TR(tb_.t[:, 0:128], ckvn.t[:, i, :], ident_b.t[:, :], [ckvn.b, ident_b.b], [tb_.b])
                TR(tb_.t[:, 128:256], kpe2.t[:, i, 0:128], ident_b.t[:, :], [kpe2.b, ident_b.b], [tb_.b])
                TR(tb_.t[:, 256:384], kpe2.t[:, i, 128:256], ident_b.t[:, :], [kpe2.b, ident_b.b], [tb_.b])
                CP("act", klT.t[:, i * 128:(i + 1) * 128], tb_.t[:, 0:128], [tb_.b], [klT.b])
                CP("act", kpT.t[:, :, i * 128:(i + 1) * 128], tb_.t[:, 128:384].rearrange("p (v n) -> p v n", v=2), [tb_.b], [kpT.b])
            DMA("pool", ctx_kl[:, ctx_col0:ctx_col0 + TB], klT.t[:, :], [klT.b], [ctx_b])
            DMA("pool", ctx_kp[:, :, ctx_col0:ctx_col0 + TB], kpT.t[:, :, :], [kpT.b], [ctx_b])
            DMA("pool", ctx_v[:, ctx_col0 // 128:ctx_col0 // 128 + NT, :], ckvn.t[:, :, :], [ckvn.b], [ctx_b])
            mb_ap = None
            if mask_cols is not None:
                mb_ap = cmask_t.t[:, mask_cols[0]:mask_cols[0] + 1]
                TS("dve", Pdec.t[:, :], Pdec.t[:, :], cmask_t.t[:, mask_cols[1]:mask_cols[1] + 1], None, ALU.mult, None,
                   [Pdec.b, cmask_t.b], [Pdec.b])
            if fwd_acc:
                for i in reversed(range(NT)):
                    gates_multi([(i, 1, 1), (i, 0, 0)], False)
                    ub1 = ktilde_U(i, 1)
                    ub0 = ktilde_U(i, 0)
                    state_step(ub1, 1, mb_ap)
                    TTOP("dve", Pdec.t[:, :], Pdec.t[:, :], dcy.t[:, 0, :], ALU.mult, [Pdec.b, dcy.b], [Pdec.b])
                    for hh in range(2):
                        TTOP("dve", tmpS.t[:, hh * 512:(hh + 1) * 512].rearrange("p (h n) -> p h n", h=2),
                             ub0[hh].t[:, :].rearrange("p (h n) -> p h n", h=2),
                             Pdec.t[:, hh * 2:hh * 2 + 2].unsqueeze(2).to_broadcast([128, 2, 256]), ALU.mult,
                             [ub0[hh].b, Pdec.b], [tmpS.b])
                    TTOP("dve", Sst[0].t[:, :], Sst[0].t[:, :], tmpS.t[:, :], ALU.add, [Sst[0].b, tmpS.b], [Sst[0].b])
            else:
                for i0 in reversed(range(0, NT, 2)):
                    gates_multi([(i0 + 1, 1, 0), (i0, 1, 1)], False)
                    uba = ktilde_U(i0 + 1, 0)
                    ubb = ktilde_U(i0, 1)
                    state_step(uba, 1, mb_ap, 0)
                    state_step(ubb, 1, mb_ap, 1)

        def passB_block(xsrc, row0, rope_src, tile0, nkt, bnd_idx, ydst, yrow0, pre=False, next_hprep=None, next_xload=None):
            if not pre:
                hprep(xsrc, row0)
            to_G()
            DMA("sp", ya[1].t[:, :], bnd[bnd_idx, :, :], [bnd_b[bnd_idx]], [ya[1].b])
            CP("dve", Sst[1].t[:, :], ya[1].t[:, :], [ya[1].b], [Sst[1].b])
            DMA("sp", kraw.t[:, :, :].rearrange("p a n -> p (a n)"), kv_k[bnd_idx, :, :], [kvsc_b[bnd_idx]], [kraw.b])
            DMA("sp", vtok.t[:, :, :].rearrange("p a n -> p (a n)"), kv_v[bnd_idx, :, :], [kvsc_b[bnd_idx]], [vtok.b])
            DMA("sp", graug.t[0:16, :, :].rearrange("p a n -> p (a n)"), kv_g[bnd_idx, :, :], [kvsc_b[bnd_idx]], [graug.b])
            for which, c0 in ((0, C_Q), (1, C_K)):
                wq_ = wload(wsc_v, c0, 512, wsc_bs)
                for h in range(4):
                    proj_feat(wq_, h * 128, 128, lambda a, pb: CP("act", qkraw.t[:, a, :], pb.t[:, 0:TB], [pb.b], [qkraw.b]), which * 4 + h)
            for step in range(NT):
                items = [(step, 0, 0), (NT - 1 - step, 1, 1)]
                gates_multi(items, True)
                for (i, dirn, sl_) in items:
                    sl = slice(i * 128, (i + 1) * 128)
                    STT("dve", qkt.t[:, dirn * 2, :, sl], qkraw.t[:, 0:4, sl], 128.0 ** -0.5, etns[sl_].t[:, :, :], ALU.mult, ALU.mult,
                        [qkraw.b, etns[sl_].b], [qkt.b])
                    TTOP("dve", qkt.t[:, dirn * 2 + 1, :, sl], qkraw.t[:, 4:8, sl], etps[sl_].t[:, :, :], ALU.mult, [qkraw.b, etps[sl_].b], [qkt.b])
                    CP("act", sbf.t[:, dirn, i, :], Sst[dirn].t[:, :], [Sst[dirn].b], [sbf.b])
                ubs = [ktilde_U(i, sl_) for (i, dirn, sl_) in items]
                for (i, dirn, sl_), ub in zip(items, ubs):
                    state_step(ub, dirn, None, sl_)
            wz = [wload(wsc_v, C_ZA, 512, wsc_bs), wload(wsc_v, C_ZA + 512, 512, wsc_bs)]
            ez = e1s[1]
            obs = {}

            def out_A(i):
                sl = slice(i * 128, (i + 1) * 128)
                pas = []
                for dirn in range(2):
                    pa = bank()
                    pas.append(pa)
                    for h in range(4):
                        MM(pa.t[:, h * 128:(h + 1) * 128], qkt.t[:, dirn * 2 + 1, h, sl], qkt.t[:, dirn * 2, h, sl], True, True, [qkt.b], [pa.b])
                pzs = []
                for hv in range(2):
                    pz = bank()
                    pzs.append(pz)
                    for dt in range(8):
                        MM(pz.t[:, :], hT.t[:, dt, sl], wz[hv].t[:, dt, :], dt == 0, dt == 7, [hT.b, wz[hv].b], [pz.b])
                for dirn in range(2):
                    TTOP("dve", atT[dirn].t[:, :, :], pas[dirn].t[:, :].rearrange("p (h n) -> p h n", h=4),
                         maskb.t[:, dirn, :].unsqueeze(1).to_broadcast([128, 4, 128]), ALU.mult, [pas[dirn].b, maskb.b], [atT[dirn].b])
                for hv in range(2):
                    pz = pzs[hv]
                    ACT(ez.t[:, :], pz.t[:, :], AF.Tanh, [pz.b], [ez.b], scale=0.5)
                    STT("dve", ez.t[:, :], ez.t[:, :], 1.0, pz.t[:, :], ALU.add, ALU.mult, [ez.b, pz.b], [ez.b])
                    STT("dve", zg.t[:, hv * 512:(hv + 1) * 512], ez.t[:, :], 0.5, gla_g_bc[:, hv * 512:(hv + 1) * 512], ALU.mult, ALU.mult,
                        [ez.b, vec_t.b], [zg.b])

            def out_B(i):
                sl = slice(i * 128, (i + 1) * 128)
                ob = [bank(), bank()]
                for h in range(4):
                    o_ap = ob[h // 2].t[:, (h % 2) * 256:(h % 2) * 256 + 256]
                    vh = vtok.t[:, i, h * 256:(h + 1) * 256]
                    MM(o_ap, atT[0].t[:, h, :], vh, True, False, [atT[0].b, vtok.b], [ob[h // 2].b])
                    MM(o_ap, atT[1].t[:, h, :], vh, False, False, [atT[1].b, vtok.b], [ob[h // 2].b])
                    MM(o_ap, qkt.t[:, 0, h, sl], sbf.t[:, 0, i, h * 256:(h + 1) * 256], False, False, [qkt.b, sbf.b], [ob[h // 2].b])
                    MM(o_ap, qkt.t[:, 2, h, sl], sbf.t[:, 1, i, h * 256:(h + 1) * 256], False, True, [qkt.b, sbf.b], [ob[h // 2].b])
                for h in range(4):
                    o_ap = ob[h // 2].t[:, (h % 2) * 256:(h % 2) * 256 + 256]
                    ACT(junk.t[:, 0:256], o_ap, AF.Square, [ob[h // 2].b], [junk.b, ss4.b], accum_out=ss4.t[:, h:h + 1])
                rstd_from_ss(ss4.t[:, 0:4], rs4.t[:, 0:4], 256, ss4.b, rs4.b)
                y_ = ya[i % 2]
                for h in range(4):
                    o_ap = ob[h // 2].t[:, (h % 2) * 256:(h % 2) * 256 + 256]
                    STT("dve", y_.t[:, h * 256:(h + 1) * 256], o_ap, rs4.t[:, h:h + 1], zg.t[:, h * 256:(h + 1) * 256], ALU.mult, ALU.mult,
                        [ob[h // 2].b, rs4.b, zg.b], [y_.b])

            def out_C(i):
                sl = slice(i * 128, (i + 1) * 128)
                y_ = ya[i % 2]
                for half in range(2):
                    tb_ = tbank()
                    for j in range(4):
                        ft = half * 4 + j
                        TR(tb_.t[:, j * 128:(j + 1) * 128], y_.t[:, ft * 128:(ft + 1) * 128], ident_b.t[:, :], [y_.b, ident_b.b], [tb_.b])
                    CP("act", yaT.t[:, half * 4:half * 4 + 4, sl], tb_.t[:, 0:512].rearrange("p (a n) -> p a n", a=4), [tb_.b], [yaT.b])

            out_A(0)
            for i in range(NT):
                out_B(i)
                if i + 1 < NT:
                    out_A(i + 1)
                out_C(i)
            if next_xload is not None:
                next_xload()
            to_M()
            wcq = wload(wsc_v, C_CQ, 256, wsc_bs)
            DMA("sp", ropet.t[:, :, :], rope_src[tile0:tile0 + NT, :, :].rearrange("a p n -> p a n"), [], [ropet.b])

            cq_banks = {}

            def cq_sq(i, pb):
                cq_banks[i] = pb
                ACT(junk.t[:, 0:256], pb.t[:, 0:256], AF.Square, [pb.b], [junk.b, ss4.b], accum_out=ss4.t[:, 4 + i:5 + i])

            proj_tok(wcq, 256, cq_sq)
            rstd_from_ss(ss4.t[:, 4:4 + NT], rs4.t[:, 4:4 + NT], 256, ss4.b, rs4.b)
            for i in range(NT):
                pb = cq_banks[i]
                STT("dve", cqn.t[:, i, :], pb.t[:, 0:256], rs4.t[:, 4 + i:5 + i], q_g_bc, ALU.mult, ALU.mult, [pb.b, rs4.b, vec_t.b], [cqn.b])
            for i in range(NT):
                tb_ = tbank()
                for c in range(2):
                    TR(tb_.t[:, c * 128:(c + 1) * 128], cqn.t[:, i, c * 128:(c + 1) * 128], ident_b.t[:, :], [cqn.b, ident_b.b], [tb_.b])
                CP("act", cqT.t[:, :, i * 128:(i + 1) * 128], tb_.t[:, 0:256].rearrange("p (c n) -> p c n", c=2), [tb_.b], [cqT.b])
            for h in range(8):
                pb = bank()
                for c in range(2):
                    MM(pb.t[:, 0:TB], wcomb.t[:, c, h, :], cqT.t[:, c, :], c == 0, c == 1, [wcomb.b, cqT.b], [pb.b])
                CP("act", QT.t[:, h, :], pb.t[:, 0:TB], [pb.b], [QT.b])
            qp_banks = []
            for i in range(NT):
                pb = bank()
                qp_banks.append(pb)
                for c in range(2):
                    MM(pb.t[:, :], cqT.t[:, c, i * 128:(i + 1) * 128], wpe.t[:, c, :, :].rearrange("p h n -> p (h n)"), c == 0, c == 1,
                       [cqT.b, wpe.b], [pb.b])
            for i in range(NT):
                pb = qp_banks[i]
                p3 = pb.t[:, :].rearrange("p (h n) -> p h n", h=8)
                x1 = p3[:, :, 0:32]
                x2 = p3[:, :, 32:64]
                cs = ropet.t[:, i, 0:32].unsqueeze(1).to_broadcast([128, 8, 32])
                sn = ropet.t[:, i, 32:64].unsqueeze(1).to_broadcast([128, 8, 32])
                r = [rtmp[j].t[:, :, :] for j in range(4)]
                rb = [rtmp[j].b for j in range(4)]
                TTOP("dve", r[0], x1, cs, ALU.mult, [pb.b, ropet.b], [rb[0]])
                TTOP("dve", r[1], x2, sn, ALU.mult, [pb.b, ropet.b], [rb[1]])
                TTOP("dve", r[2], x2, cs, ALU.mult, [pb.b, ropet.b], [rb[2]])
                TTOP("dve", r[3], x1, sn, ALU.mult, [pb.b, ropet.b], [rb[3]])
                q3 = qpe_tok.t[:, i, :].rearrange("p (h n) -> p h n", h=8)
                TTOP("dve", q3[:, :, 0:32], r[0], r[1], ALU.subtract, [rb[0], rb[1]], [qpe_tok.b])
                TTOP("dve", q3[:, :, 32:64], r[2], r[3], ALU.add, [rb[2], rb[3]], [qpe_tok.b])
                tb_ = tbank()
                for j in range(4):
                    TR(tb_.t[:, j * 128:(j + 1) * 128], qpe_tok.t[:, i, j * 128:(j + 1) * 128], ident_b.t[:, :], [qpe_tok.b, ident_b.b], [tb_.b])
                CP("act", qpeT.t[:, :, i * 128:(i + 1) * 128], tb_.t[:, 0:512].rearrange("p (a n) -> p a n", a=4), [tb_.b], [qpeT.b])
            nch = nkt // CKT
            scale = 192.0 ** -0.5
            cq_ = {"n": 0}

            def ctx_load(ch):
                s_ = CX[cq_["n"] % 3]
                cq_["n"] += 1
                c0 = ch * CKT * 128
                DMA("pool", s_[0].t[:, :], ctx_kl[:, c0:c0 + CKT * 128], [ctx_b], [s_[0].b])
                DMA("pool", s_[1].t[:, :, :], ctx_kp[:, :, c0:c0 + CKT * 128], [ctx_b], [s_[1].b])
                DMA("pool", s_[2].t[:, :, :], ctx_v[:, ch * CKT:(ch + 1) * CKT, :], [ctx_b], [s_[2].b])
                return s_

            acc = [PB[0], PB[1], PB[2], PB[3]]
            stb = [PB[4], PB[5], PB[6], PB[7]]
            szsets = [[szs[0], szs[1]], [t1, t2]]

            def pair_prologue(hp):
                wzb = wload(wsc_v, C_ZB + hp * 256, 256, wsc_bs)
                loaded = [ctx_load(0)]
                if nch > 1:
                    loaded.append(ctx_load(1))
                for hh in range(2):
                    pz = stb[2 + hh] if hp == 0 else stb[hh]
                    for dt in range(8):
                        MM(pz.t[:, :], wzb.t[:, dt, hh * 128:(hh + 1) * 128], hT.t[:, dt, :], dt == 0, dt == 7, [wzb.b, hT.b], [pz.b])
                    sz = szsets[hp % 2][hh]
                    ACT(sz.t[:, :], pz.t[:, :], AF.Tanh, [pz.b], [sz.b], scale=0.5)
                    STT("dve", sz.t[:, :], sz.t[:, :], 1.0, pz.t[:, :], ALU.add, ALU.mult, [sz.b, pz.b], [sz.b])
                return loaded

            def pair_body(hp, loaded):
                nacc = [[0, 0], [0, 0]]
                units = []
                for ch in range(nch):
                    for j in range(CKT):
                        for hh in range(2):
                            units.append((ch, j, hh))
                pend = []
                ui = 0

                def do_pv(u, pt):
                    ch, j, hh = u
                    s_ = loaded[ch]
                    first = (ch == 0 and j == 0)
                    last = (ch == nch - 1 and j == CKT - 1)
                    MM(acc[hh * 2].t[:, :], s_[2].t[:, j, :], pt.t[:, :], first, last, [s_[2].b, pt.b], [acc[hh * 2].b])
                    pr = (ch * CKT + j) % 2
                    ra = raccs[hh][pr]
                    if nacc[hh][pr] == 0:
                        CP("dve", ra.t[:, :], pt.t[:, :], [pt.b], [ra.b])
                    else:
                        TTOP("dve", ra.t[:, :], ra.t[:, :], pt.t[:, :], ALU.add, [ra.b, pt.b], [ra.b])
                    nacc[hh][pr] += 1

                for u in units:
                    ch, j, hh = u
                    s_ = loaded[ch]
                    h = hp * 2 + hh
                    sb_ = stb[ui % 4]
                    pt = pTs[ui % 4]
                    ui += 1
                    MM(sb_.t[:, :], s_[0].t[:, j * 128:(j + 1) * 128], QT.t[:, h, :], True, False, [s_[0].b, QT.b], [sb_.b])
                    MM(sb_.t[:, :], s_[1].t[:, h % 2, j * 128:(j + 1) * 128], qpeT.t[:, h // 2, :], False, True,
                       [s_[1].b, qpeT.b], [sb_.b])
                    ACT(pt.t[:, :], sb_.t[:, :], AF.Exp, [sb_.b], [pt.b], scale=scale)
                    pend.append((u, pt))
                    if len(pend) > 2:
                        do_pv(*pend.pop(0))
                    if j == 0 and hh == 1 and ch + 2 < nch:
                        loaded.append(ctx_load(ch + 2))
                while pend:
                    do_pv(*pend.pop(0))

            def pair_epilogue(hp):
                for hh in range(2):
                    TTOP("dve", raccs[hh][0].t[:, :], raccs[hh][0].t[:, :], raccs[hh][1].t[:, :], ALU.add, [raccs[hh][0].b, raccs[hh][1].b], [raccs[hh][0].b])
                    MM(acc[hh * 2 + 1].t[:, :], ones_f, raccs[hh][0].t[:, :], True, True, [cst_t.b, raccs[hh][0].b], [acc[hh * 2 + 1].b])
                for hh in range(2):
                    S.op("dve", lambda e, a=acc[hh * 2 + 1]: e.reciprocal(out=rinv.t[:, :], in_=a.t[:, :]), [acc[hh * 2 + 1].b], [rinv.b])
                    ol = olat[hh]
                    TTOP("dve", ol.t[:, :], acc[hh * 2].t[:, :], rinv.t[:, :], ALU.mult, [acc[hh * 2].b, rinv.b], [ol.b])
                for hh in range(2):
                    h = hp * 2 + hh
                    pob = stb[2 + hh]
                    sz = szsets[hp % 2][hh]
                    MM(pob.t[:, :], wuv.t[:, h, :], olat[hh].t[:, :], True, True, [wuv.b, olat[hh].b], [pob.b])
                    STT("dve", ybT.t[:, h, :], pob.t[:, :], 0.5, sz.t[:, :], ALU.mult, ALU.mult, [pob.b, sz.b], [ybT.b])

            loaded_cur = pair_prologue(0)
            for hp in range(4):
                pair_body(hp, loaded_cur)
                if hp + 1 < 4:
                    loaded_cur = pair_prologue(hp + 1)
                pair_epilogue(hp)
            for ot in range(8):
                if ot % 2 == 0:
                    wg_ = wload2(wsc_v, C_GA + ot * 128, wsc_v, C_GBT + ot * 128, wsc_bs, wsc_bs)
                    wab_ = wload2(wabo_v[0], ot * 128, wabo_v[1], ot * 128, wabo_bs, wabo_bs)
                oc = (ot % 2) * 128
                banks = [bank() for _ in range(4)]
                for dt in range(8):
                    MM(banks[0].t[:, :], wg_.t[:, dt, oc:oc + 128], hT.t[:, dt, :], dt == 0, dt == 7, [wg_.b, hT.b], [banks[0].b])
                for dt in range(8):
                    MM(banks[1].t[:, :], wg_.t[:, dt, 256 + oc:256 + oc + 128], hT.t[:, dt, :], dt == 0, dt == 7, [wg_.b, hT.b], [banks[1].b])
                for ft in range(8):
                    MM(banks[2].t[:, :], wab_.t[:, ft, oc:oc + 128], yaT.t[:, ft, :], ft == 0, ft == 7, [wab_.b, yaT.b], [banks[2].b])
                for ft in range(8):
                    MM(banks[3].t[:, :], wab_.t[:, ft, 256 + oc:256 + oc + 128], ybT.t[:, ft, :], ft == 0, ft == 7, [wab_.b, ybT.b], [banks[3].b])
                ACT(t1.t[:, :], banks[0].t[:, :], AF.Tanh, [banks[0].b], [t1.b], scale=0.5)
                ACT(t2.t[:, :], banks[1].t[:, :], AF.Tanh, [banks[1].b], [t2.b], scale=0.5)
                STT("dve", t1.t[:, :], t1.t[:, :], 1.0, banks[2].t[:, :], ALU.add, ALU.mult, [t1.b, banks[2].b], [t1.b])
                STT("dve", t2.t[:, :], t2.t[:, :], 1.0, banks[3].t[:, :], ALU.add, ALU.mult, [t2.b, banks[3].b], [t2.b])
                TTOP("dve", t1.t[:, :], t1.t[:, :], t2.t[:, :], ALU.add, [t1.b, t2.b], [t1.b])
                ACT(mrgT.t[:, ot, :], t1.t[:, :], AF.Copy, [t1.b], [mrgT.b], scale=0.5)
            if next_hprep is not None:
                next_hprep()
            wo = [wload(wabo_v[2], 0, 512, wabo_bs), wload(wabo_v[2], 512, 512, wabo_bs)]
            for i in range(NT):
                xt = xsl[xq["n"] % 2]
                xq["n"] += 1
                DMA("pool", xt.t[:, :], xsrc[row0 + i * 128:row0 + (i + 1) * 128, :], [], [xt.b])
                y_ = yo[i % 2]
                obk = [bank(), bank()]
                for hv in range(2):
                    for ft in range(8):
                        MM(obk[hv].t[:, :], mrgT.t[:, ft, i * 128:(i + 1) * 128], wo[hv].t[:, ft, :], ft == 0, ft == 7, [mrgT.b, wo[hv].b], [obk[hv].b])
                    ACT(junk.t[:, 0:512], obk[hv].t[:, :], AF.Square, [obk[hv].b], [junk.b, ss4.b], accum_out=ss4.t[:, hv:hv + 1])
                TTOP("dve", ss4.t[:, 2:3], ss4.t[:, 0:1], ss4.t[:, 1:2], ALU.add, [ss4.b], [ss4.b])
                rstd_from_ss(ss4.t[:, 2:3], rs4.t[:, 2:3], D, ss4.b, rs4.b)
                for hv in range(2):
                    STT("dve", y_.t[:, hv * 512:(hv + 1) * 512], obk[hv].t[:, :], rs4.t[:, 2:3], post_g_bc[:, hv * 512:(hv + 1) * 512],
                        ALU.mult, ALU.mult, [obk[hv].b, rs4.b, vec_t.b], [y_.b])
                TTOP("pool", y_.t[:, :], y_.t[:, :], xt.t[:, :], ALU.add, [y_.b, xt.b], [y_.b])
                ev = DMA("pool", ydst[yrow0 + i * 128:yrow0 + (i + 1) * 128, :], y_.t[:, :], [y_.b], [Buf()])
                out_events.append(ev)

        def zero_states():
            S.op("pool", lambda e: e.memset(Sst[0].t[:, :], 0.0), [], [Sst[0].b])
            S.op("pool", lambda e: e.memset(Sst[1].t[:, :], 0.0), [], [Sst[1].b])

        tasks = []
        for s_i in range(NP if cfg.get("DO_PROMPT", True) else 0):
            base = s_i * SP
            tasks.append(("zero", None))
            for blk in reversed(range(NBP)):
                tasks.append((passA_block, (xp, base + blk * TB, ropep, blk * NT, blk * TB, blk, False, None)))
            tasks.append(("zerof", None))
            for blk in range(NBP):
                tasks.append((passB_block, (xp, base + blk * TB, ropep, blk * NT, SP // 128, blk, yp, base + blk * TB)))
        if cfg.get("DO_SAMPLE", True):
            tasks.append(("zero", None))
            tasks.append(("onesP", None))
            for blk in reversed(range(NBS, NBS + NBC)):
                j = blk - NBS
                tasks.append((passA_block, (xs, blk * TB, ropes, blk * NT, blk * TB, None, True, (j, NBC + j))))
            for blk in reversed(range(NBS)):
                tasks.append((passA_block, (xs, blk * TB, ropes, blk * NT, blk * TB, blk, False, None)))
            for blk in range(NBS):
                tasks.append((passB_block, (xs, blk * TB, ropes, blk * NT, SS // 128, blk, ys, blk * TB)))
        blocks = [i for i, t in enumerate(tasks) if not isinstance(t[0], str)]
        pre_done = False
        for ti, (fn, args) in enumerate(tasks):
            if fn == "zero":
                zero_states()
            elif fn == "zerof":
                S.op("pool", lambda e: e.memset(Sst[0].t[:, :], 0.0), [], [Sst[0].b])
            elif fn == "onesP":
                S.op("pool", lambda e: e.memset(Pdec.t[:, :], 1.0), [], [Pdec.b])
            else:
                nxt = None
                nxl = None
                later = [i for i in blocks if i > ti]
                if later and cfg.get("PREFETCH_H", True):
                    nargs = tasks[later[0]][1]
                    box = {}
                    nxl = (lambda a=nargs, box=box: box.__setitem__("s", xload_pair(a[0], a[1], 0)))
                    nxt = (lambda a=nargs, box=box: hprep(a[0], a[1], box.get("s")))
                fn(*args, pre=pre_done, next_hprep=nxt, next_xload=nxl)
                pre_done = nxt is not None
        print("ops:", {e: len(S.ops[e]) for e in ENGS}, "sbuf free", nc.sbuf_bytes_remaining)

        S.emit(out_events)
    return nc


def rope_table(pos):
    half = 32
    freqs = (np.float32(10000.0) ** (-np.arange(half, dtype=np.float32) / np.float32(half))).astype(np.float32)
    ang = pos.astype(np.float32)[:, None] * freqs[None, :]
    return np.concatenate([np.cos(ang), np.sin(ang)], axis=-1).astype(np.float32)


def make_consts():
    ident = np.eye(128, dtype=np.float32)
    s = np.arange(128)[:, None]
    t = np.arange(128)[None, :]
    tri_f = (s <= t).astype(np.float32) / 16.0
    tri_b = (s >= t).astype(np.float32) / 16.0
    mask_f = (s <= t).astype(np.float32)
    mask_b = (s > t).astype(np.float32)
    return np.concatenate([ident, tri_f, tri_b, mask_f, mask_b, np.ones((128, 128), np.float32)], axis=1)


_NC_CACHE = {}


def run(cfg, x_prompt, x_sample, pre_norm_g, w_in, gla_wg2_fwd, gla_bg2_fwd, gla_wg2_bwd, gla_bg2_bwd,
        gla_out_norm_g, mla_q_norm_g, mla_w_uq, mla_kv_norm_g, mla_w_ukv, w_branch_a, w_branch_b,
        w_out, post_norm_g):
    NP, SP, SS, TB = cfg["NP"], cfg["SP"], cfg["SS"], cfg["TB"]
    SL = SS // NCORE
    NBS = SL // TB
    NBC = (SS - SL) // TB
    key = (NP, SP, SS, TB, cfg.get("DO_PROMPT", True), cfg.get("DO_SAMPLE", True))
    if key not in _NC_CACHE:
        _NC_CACHE[key] = build(cfg)
    nc = _NC_CACHE[key]
    f = lambda a: np.ascontiguousarray(np.asarray(a, dtype=np.float32))
    x_prompt = f(x_prompt)
    x_sample = f(x_sample)
    assert x_prompt.shape == (NP * NCORE, SP, D) and x_sample.shape == (1, SS, D)
    wg2 = np.stack([np.concatenate([f(gla_wg2_fwd)[0], f(gla_bg2_fwd)[0][None, :]], 0),
                    np.concatenate([f(gla_wg2_bwd)[0], f(gla_bg2_bwd)[0][None, :]], 0)], 0)
    vecs = np.concatenate([f(post_norm_g)[0], np.tile(f(gla_out_norm_g)[0], 4), f(mla_q_norm_g)[0], f(mla_kv_norm_g)[0]])
    common = {
        "w_in": f(w_in)[0], "wg2": f(wg2), "w_uq": f(mla_w_uq)[0], "w_ukv": f(mla_w_ukv)[0],
        "w_a": f(w_branch_a)[0], "w_b": f(w_branch_b)[0], "w_o": f(w_out)[0],
        "pre_g": np.ascontiguousarray(f(pre_norm_g)[0].reshape(8, 128).T),
        "vecs": f(vecs),
        "ropep": rope_table(np.arange(SP)).reshape(SP // 128, 128, 64),
        "cst": make_consts(),
    }
    in_maps = []
    for c in range(NCORE):
        m = dict(common)
        m["xp"] = x_prompt[c * NP:(c + 1) * NP].reshape(NP * SP, D)
        roll = (np.arange(SS) + c * SL) % SS
        m["xs"] = np.ascontiguousarray(x_sample[0][roll])
        m["ropes"] = rope_table(roll).reshape(SS // 128, 128, 64)
        NB = SS // TB
        mb = np.ones(max(NBC, 1), np.float32)
        mf = np.ones(max(NBC, 1), np.float32)
        for j in range(NBC):
            ob = (c * NBS + NBS + j) % NB
            after = ob >= (c + 1) * NBS
            mb[j] = 1.0 if after else 0.0
            mf[j] = 0.0 if after else 1.0
        m["cmask"] = np.ascontiguousarray(np.broadcast_to(np.concatenate([mb, mf])[None, :], (128, 2 * max(NBC, 1)))).astype(np.float32)
        in_maps.append(m)
    if cfg.get("RETURN_MAPS"):
        return nc, in_maps
    res = run_bass_kernel_spmd(nc, in_maps, core_ids=list(range(NCORE)))
    yp = np.stack([r["yp"] for r in res.results], 0).reshape(NP * NCORE, SP, D)
    ys = np.concatenate([r["ys"] for r in res.results], 0).reshape(1, SS, D)
    return yp.astype(np.float32), ys.astype(np.float32)


def kernel(**inputs):
    return run(FULL_CFG, **inputs)
```
